# Optimizing a Trainium2 kernel written in Bass

```python
import math
import jax, jax.numpy as jnp
from jax import lax
import numpy as np

D_MODEL = 1024
BATCH = 8
SEQ = 8192
DEPTH = 1

EPS = 1e-6
HEAD_DIM = 64
N_Q_HEADS = D_MODEL // 64
N_KV_HEADS = max(1, N_Q_HEADS // 8)
GROUP = N_Q_HEADS // N_KV_HEADS
WINDOW = 128
BLOCK = 128
RET_HEADS = D_MODEL // 256
RET_QK_DIM = 256
RET_V_DIM = 512
RET_CHUNK = 128
RET_ROT_BASE = 10000.0
D_FF = ((-(-8 * D_MODEL // 3) + 255) // 256) * 256

ATT_Q = N_Q_HEADS * HEAD_DIM
ATT_KV = N_KV_HEADS * HEAD_DIM
RET_QK = RET_HEADS * RET_QK_DIM
RET_V = RET_HEADS * RET_V_DIM
IN_SPLITS = (ATT_Q, ATT_KV, ATT_KV, RET_QK, RET_QK, RET_V, RET_V, D_MODEL, D_MODEL)
D_IN = sum(IN_SPLITS)

kernel_name = "hybrid_swa_sink_retention_gated_block"


def rmsnorm(x, gain):
    xf = x.astype(jnp.float32)
    xf = xf * lax.rsqrt(jnp.mean(xf * xf, axis=-1, keepdims=True) + EPS)
    return (xf * gain.astype(jnp.float32)).astype(x.dtype)


def rms_group_norm(x):
    xf = x.astype(jnp.float32)
    return xf * lax.rsqrt(jnp.mean(xf * xf, axis=-1, keepdims=True) + EPS)


def sliding_window_attention(q, k, v, q_gain, k_gain, sinks):
    b, s, _, _ = q.shape
    nb = s // BLOCK
    q = rmsnorm(q, q_gain)
    k = rmsnorm(k, k_gain)
    qb = q.reshape(b, nb, BLOCK, N_KV_HEADS, GROUP, HEAD_DIM)
    kb = k.reshape(b, nb, BLOCK, N_KV_HEADS, HEAD_DIM)
    vb = v.reshape(b, nb, BLOCK, N_KV_HEADS, HEAD_DIM)
    pad = ((0, 0), (1, 0), (0, 0), (0, 0), (0, 0))
    k_band = jnp.concatenate([jnp.pad(kb, pad)[:, :-1], kb], axis=2)
    v_band = jnp.concatenate([jnp.pad(vb, pad)[:, :-1], vb], axis=2)
    scores = jnp.einsum('bnqkgd,bnskd->bnkgqs', qb, k_band).astype(jnp.float32) * (HEAD_DIM ** -0.5)
    blk = jnp.arange(nb)[:, None, None]
    q_pos = blk * BLOCK + jnp.arange(BLOCK)[None, :, None]
    k_pos = blk * BLOCK - BLOCK + jnp.arange(2 * BLOCK)[None, None, :]
    allowed = (k_pos <= q_pos) & (k_pos > q_pos - WINDOW) & (k_pos >= 0)
    scores = jnp.where(allowed[None, :, None, None], scores, -jnp.inf)
    sink = sinks.astype(jnp.float32).reshape(N_KV_HEADS, GROUP)[None, None, :, :, None, None]
    sink = jnp.broadcast_to(sink, scores.shape[:-1] + (1,))
    probs = jax.nn.softmax(jnp.concatenate([scores, sink], axis=-1), axis=-1)[..., :-1]
    out = jnp.einsum('bnkgqs,bnskd->bnqkgd', probs.astype(v.dtype), v_band)
    return out.reshape(b, s, N_Q_HEADS * HEAD_DIM)


def rotate_every_two(x):
    x1 = x[..., 0::2]
    x2 = x[..., 1::2]
    return jnp.stack((-x2, x1), axis=-1).reshape(x.shape)


def retention_chunkwise(q, k, v):
    b, s, _, _ = q.shape
    n = s // RET_CHUNK
    q = q.astype(jnp.float32)
    k = k.astype(jnp.float32) * (RET_QK_DIM ** -0.5)
    v = v.astype(jnp.float32)
    pos = jnp.arange(s, dtype=jnp.float32)
    theta = 1.0 / (RET_ROT_BASE ** jnp.linspace(0.0, 1.0, RET_QK_DIM // 2, dtype=jnp.float32))
    ang = jnp.repeat(pos[:, None] * theta[None, :], 2, axis=-1)[None, :, None, :]
    cos, sin = jnp.cos(ang), jnp.sin(ang)
    q = q * cos + rotate_every_two(q) * sin
    k = k * cos + rotate_every_two(k) * sin
    log_gamma = jnp.log(1.0 - 2.0 ** (-5.0 - jnp.arange(RET_HEADS, dtype=jnp.float32)))
    i = jnp.arange(RET_CHUNK, dtype=jnp.float32)
    diff = i[:, None] - i[None, :]
    causal = diff >= 0
    decay_inner = jnp.where(causal[None], jnp.exp(jnp.where(causal, diff, 0.0)[None] * log_gamma[:, None, None]), 0.0)
    xi = jnp.exp((i + 1.0)[None, :] * log_gamma[:, None])
    zeta = jnp.exp((RET_CHUNK - 1.0 - i)[None, :] * log_gamma[:, None])
    gamma_chunk = jnp.exp(RET_CHUNK * log_gamma)

    def to_chunks(t):
        return t.reshape(b, n, RET_CHUNK, RET_HEADS, t.shape[-1]).transpose(1, 0, 3, 2, 4)

    def step(state, qkv):
        qc, kc, vc = qkv
        inner = jnp.einsum('bhid,bhjd->bhij', qc, kc) * decay_inner
        out = jnp.einsum('bhij,bhjv->bhiv', inner, vc)
        out = out + jnp.einsum('bhid,bhdv->bhiv', qc, state) * xi[None, :, :, None]
        state = gamma_chunk[None, :, None, None] * state + jnp.einsum('bhjd,bhjv->bhdv', kc * zeta[None, :, :, None], vc)
        return state, out

    state0 = jnp.zeros((b, RET_HEADS, RET_QK_DIM, RET_V_DIM), jnp.float32)
    _, out = lax.scan(step, state0, (to_chunks(q), to_chunks(k), to_chunks(v)))
    return out.transpose(1, 0, 3, 2, 4).reshape(b, s, RET_HEADS, RET_V_DIM)


def setup_inputs(seed: int = 0) -> dict:
    key = jax.random.key(seed)
    ks = jax.random.split(key, 14)
    L = DEPTH
    nrm = lambda k, shape, fan_in: jax.random.normal(k, shape, jnp.float32) * (fan_in ** -0.5)
    gain = lambda k, shape: 1.0 + 0.02 * jax.random.normal(k, shape, jnp.float32)
    return {
        "x": jax.random.normal(ks[0], (BATCH, SEQ, D_MODEL), jnp.float32),
        "norm_mix_gain": gain(ks[1], (L, D_MODEL)),
        "w_in": nrm(ks[2], (L, D_MODEL, D_IN), D_MODEL),
        "q_norm_gain": gain(ks[3], (L, HEAD_DIM)),
        "k_norm_gain": gain(ks[4], (L, HEAD_DIM)),
        "attn_sinks": 0.5 * jax.random.normal(ks[5], (L, N_Q_HEADS), jnp.float32),
        "w_branch_attn": nrm(ks[6], (L, ATT_Q, D_MODEL), ATT_Q),
        "w_branch_ret": nrm(ks[7], (L, RET_V, D_MODEL), RET_V),
        "w_out": nrm(ks[8], (L, D_MODEL, D_MODEL), D_MODEL),
        "norm_ffn_gain": gain(ks[9], (L, D_MODEL)),
        "w_ffn_gate": nrm(ks[10], (L, D_MODEL, D_FF), D_MODEL),
        "w_ffn_up": nrm(ks[11], (L, D_MODEL, D_FF), D_MODEL),
        "w_ffn_down": nrm(ks[12], (L, D_FF, D_MODEL), D_FF),
    }


def reference(x, norm_mix_gain, w_in, q_norm_gain, k_norm_gain, attn_sinks, w_branch_attn, w_branch_ret, w_out, norm_ffn_gain, w_ffn_gate, w_ffn_up, w_ffn_down):
    b, s, _ = x.shape
    split_points = list(np.cumsum(IN_SPLITS)[:-1])
    for l in range(DEPTH):
        h = rmsnorm(x, norm_mix_gain[l])
        proj = jnp.einsum('bsd,de->bse', h, w_in[l])
        q_a, k_a, v_a, q_r, k_r, v_r, g_r, z_a, z_r = jnp.split(proj, split_points, axis=-1)
        attn = sliding_window_attention(
            q_a.reshape(b, s, N_Q_HEADS, HEAD_DIM),
            k_a.reshape(b, s, N_KV_HEADS, HEAD_DIM),
            v_a.reshape(b, s, N_KV_HEADS, HEAD_DIM),
            q_norm_gain[l], k_norm_gain[l], attn_sinks[l])
        ret = retention_chunkwise(
            q_r.reshape(b, s, RET_HEADS, RET_QK_DIM),
            k_r.reshape(b, s, RET_HEADS, RET_QK_DIM),
            v_r.reshape(b, s, RET_HEADS, RET_V_DIM))
        ret = (jax.nn.silu(g_r.astype(jnp.float32)) * rms_group_norm(ret).reshape(b, s, RET_V)).astype(x.dtype)
        branch_a = jnp.einsum('bse,ed->bsd', attn, w_branch_attn[l])
        branch_r = jnp.einsum('bse,ed->bsd', ret, w_branch_ret[l])
        merged = jax.nn.sigmoid(z_a) * branch_a + jax.nn.sigmoid(z_r) * branch_r
        x = x + jnp.einsum('bsd,de->bse', merged, w_out[l])
        h = rmsnorm(x, norm_ffn_gain[l])
        gate = jnp.einsum('bsd,df->bsf', h, w_ffn_gate[l])
        up = jnp.einsum('bsd,df->bsf', h, w_ffn_up[l])
        x = x + jnp.einsum('bsf,fd->bsd', jax.nn.silu(gate) * up, w_ffn_down[l])
    return x
```

```python
import numpy as np
import contextlib
import concourse.bass as bass
import concourse.mybir as mybir
from concourse.bass_utils import run_bass_kernel_spmd

F32 = mybir.dt.float32
BF16 = mybir.dt.bfloat16
AF = mybir.ActivationFunctionType
ALU = mybir.AluOpType

D = 1024
SEQ = 8192
NCORES = 8
DFF = 2816
DIN = 9472
EPS = 1e-6
TCH = 4
NSLOT = 5
SLOTW = 4096
RET_H = 4
GAMMA = [1.0 - 2.0 ** (-5.0 - h) for h in range(RET_H)]

O_QA, O_KA, O_VA, O_QR, O_KR, O_VR, O_GR, O_ZA, O_ZR = 0, 1024, 1152, 1280, 2304, 3328, 5376, 7424, 8448


class Res:
    __slots__ = ("name", "writer", "readers")

    def __init__(self, name):
        self.name = name
        self.writer = None
        self.readers = []


class DSem:
    def __init__(self, name):
        self.name = name
        self.count = 0
        self.handle = None


class Op:
    __slots__ = ("eng", "fn", "deps", "idx", "dsem", "dval", "needed")


class Sched:
    ENGS = ("pe", "act", "dve", "pool", "sp")

    def __init__(self):
        self.streams = {e: [] for e in self.ENGS}
        self.dsems = []

    def dsem(self, name):
        s = DSem(name)
        self.dsems.append(s)
        return s

    def op(self, eng, fn, reads=(), writes=(), dsem=None):
        o = Op()
        o.eng = eng
        o.fn = fn
        o.idx = len(self.streams[eng])
        o.needed = False
        deps = []
        for r in reads:
            if r.writer is not None:
                deps.append(r.writer)
        for w in writes:
            if w.writer is not None:
                deps.append(w.writer)
            deps.extend(w.readers)
        o.deps = deps
        if dsem is not None:
            dsem.count += 16
            o.dsem = dsem
            o.dval = dsem.count
            tok = ("d", dsem, dsem.count)
        else:
            o.dsem = None
            o.dval = 0
            tok = ("e", eng, o.idx, o)
        for w in writes:
            w.writer = tok
            w.readers = []
        for r in reads:
            if dsem is None:
                r.readers = [x for x in r.readers if not (x[0] == "e" and x[1] == eng)]
            r.readers.append(tok)
        self.streams[eng].append(o)
        return o

    def finalize(self):
        for e in self.ENGS:
            for o in self.streams[e]:
                for d in o.deps:
                    if d[0] == "e":
                        if d[1] == "pe" and o.eng == "pe":
                            continue
                        d[3].needed = True
        self.cum = {}
        for e in self.ENGS:
            c = 0
            arr = []
            for o in self.streams[e]:
                if o.needed:
                    c += 1
                arr.append(c)
            self.cum[e] = arr

    def emit_engine(self, eng, engine_obj, esems):
        waited = {}
        for o in self.streams[eng]:
            need = {}
            for d in o.deps:
                if d[0] == "e":
                    if d[1] == "pe" and eng == "pe":
                        continue
                    key = ("e", d[1])
                    val = self.cum[d[1]][d[2]]
                else:
                    key = ("d", d[1])
                    val = d[2]
                if val > need.get(key, 0):
                    need[key] = val
            todo = []
            for key, val in need.items():
                if waited.get(key, 0) >= val:
                    continue
                waited[key] = val
                sem = esems[key[1]] if key[0] == "e" else key[1].handle
                todo.append((sem, val))
            for sem, val in todo[:-1]:
                engine_obj.wait_ge(sem, val)
            ins = o.fn(engine_obj)
            if todo:
                ins._wait_ge(todo[-1][0], todo[-1][1])
            if o.dsem is not None:
                ins.then_inc(o.dsem.handle, 16)
            elif o.needed:
                ins.then_inc(esems[eng], 1)


def make_units():
    units = []

    def full(src, row0, nk, col0, width=512):
        return (nk * width, [(0, nk, width, 0, width, src, row0, col0)])

    units.append(full("w_in", 0, 8, O_QA))
    units.append(full("w_in", 0, 8, O_QA + 512))
    pieces = []
    for j, c0 in enumerate((O_KA, O_KA, O_KA + 64, O_KA + 64)):
        pieces.append((0, 8, 512, j * 64, 64, "w_in", 0, c0))
    pieces.append((0, 8, 512, 256, 128, "w_in", 0, O_VA))
    pieces.append((0, 8, 512, 384, 128, "w_in", 0, O_VA))
    units.append((8 * 512, pieces))
    for g in range(2):
        units.append(full("w_in", 0, 8, O_QR + 512 * g))
    for g in range(2):
        units.append(full("w_in", 0, 8, O_KR + 512 * g))
    for g in range(4):
        units.append(full("w_in", 0, 8, O_VR + 512 * g))
    for g in range(4):
        units.append(full("w_in", 0, 8, O_GR + 512 * g))
    for m in range(8):
        units.append((24 * 128, [
            (0, 8, 128, 0, 128, "w_in", 0, O_ZA + 128 * m),
            (1024, 8, 128, 0, 128, "w_ba", 0, 128 * m),
            (2048, 8, 128, 0, 128, "w_in", 0, O_ZR + 128 * m),
        ]))
        units.append((16 * 128, [(0, 16, 128, 0, 128, "w_br", 0, 128 * m)]))
    for hf in range(2):
        units.append(full("w_o", 0, 8, 512 * hf))
    for u in range(11):
        pieces = []
        for ff in range(2):
            f = 2 * u + ff
            pieces.append((ff * 2048, 8, 128, 0, 128, "w_g", 0, 128 * f))
            pieces.append((ff * 2048 + 1024, 8, 128, 0, 128, "w_u", 0, 128 * f))
        units.append((4096, pieces))
    for hf in range(2):
        for fg in range(3):
            nk = 8 if fg < 2 else 6
            units.append(full("w_d", fg * 1024, nk, 512 * hf))
    return units


def build(ntiles=16, debug=False):
    nc = bass.Bass("TRN2", target_bir_lowering=False)
    nch = ntiles * TCH
    ntok = nch * 128

    def din(name, shape, dt=F32):
        return nc.dram_tensor(name, list(shape), dt, kind="ExternalInput").ap()

    x_d = din("x", [ntok, D])
    wsrc = {
        "w_in": din("w_in", [D, DIN]),
        "w_ba": din("w_ba", [D, D]),
        "w_br": din("w_br", [2 * D, D]),
        "w_o": din("w_o", [D, D]),
        "w_g": din("w_g", [D, DFF]),
        "w_u": din("w_u", [D, DFF]),
        "w_d": din("w_d", [DFF, D]),
    }
    gT_d = din("gT", [128, 16])
    qkg_d = din("qkg", [128, 2])
    sinks_d = din("sinks", [128, 16])
    cs_d = din("cs", [ntok, 256])
    sqk_d = din("sqk", [128, 8])
    cf_d = din("cf", [128, 4, 128])
    out_d = nc.dram_tensor("out", [ntok, D], F32, kind="ExternalOutput").ap()

    units = make_units()
    NU = len(units)
    scr = nc.dram_tensor("wscr", [NU, 128, SLOTW], BF16, kind="Internal").ap()

    S = Sched()
    es = contextlib.ExitStack()

    def sb(name, shape, dt):
        return es.enter_context(nc.sbuf_tensor(name, list(shape), dt))

    def ps(name, shape, dt):
        return es.enter_context(nc.psum_tensor(name, list(shape), dt))

    with es:
        ring = sb("ring", [128, NSLOT, SLOTW], BF16)
        NXS = 6
        xt = sb("xt", [128, NXS, D], F32)
        hT = sb("hT", [128, 8, 512], BF16)
        qaT = sb("qaT", [128, 8, 512], BF16)
        kT = sb("kT", [128, 2, 2, 640], BF16)
        vaug = sb("vaug", [128, 5, 2, 65], BF16)
        qrT = sb("qrT", [128, 8, 512], BF16)
        krT = sb("krT", [128, 8, 512], BF16)
        big = sb("big", [128, 12288], BF16)
        attnT = sb("attnT", [128, 8, 512], BF16)
        retT = sb("retT", [128, 16, 512], BF16)
        Pb4 = sb("Pb4", [128, 2, 2, 2, 512], BF16)
        Wst = sb("Wst", [128, 8, 512], F32)
        Db = sb("Db", [128, 8, 512], BF16)
        t32 = sb("t32", [128, 5, 512], F32)
        t16 = sb("t16", [128, 3, 1024], BF16)
        csb = sb("csb", [128, 2, 256], F32)
        cb = sb("cb", [128, 4, 128], BF16)
        gT = sb("gT_sb", [128, 16], F32)
        qkg = sb("qkg_sb", [128, 2], F32)
        esink = sb("esink", [128, 16], F32)
        sqk = sb("sqk_sb", [128, 8], F32)
        cst = sb("cst", [128, 4], F32)
        st = sb("st", [128, 32], F32)
        st2 = sb("st2", [128, 32], F32)
        innb = sb("innb", [128, 512], BF16)
        R_inn = Res("inn")
        st_den = st2[:, 0:16]
        st_rec = st2[:, 16:32]
        R_stden, R_strec = Res("stden"), Res("strec")

        vr = big[:, 0:8192].rearrange("p (c f) -> p c f", c=4)
        krTM = big[:, 8192:12288].rearrange("p (c f) -> p c f", c=4)
        actT = big[:, 0:22 * 512].rearrange("p (f t) -> p f t", f=22)

        NPF = 4
        pf = [ps(f"pf{i}", [128, 512], F32) for i in range(NPF)]
        pvA = ps("pvA", [128, 512], F32)
        pvB = ps("pvB", [128, 512], F32)
        pb = [ps(f"pb{i}", [128, 512], BF16) for i in range(2)]
        pvbanks = [pvA, pvB]
        R_pv = [Res("pvA"), Res("pvB")]

        R_ring = [Res(f"ring{i}") for i in range(NSLOT)]
        R_xt = [Res(f"xt{c}") for c in range(6)]
        R_hT = [Res(f"hT{c}") for c in range(TCH)]
        R_qaT = [Res(f"qaT{e}") for e in range(8)]
        R_kTc, R_kTp = Res("kTc"), Res("kTp")
        R_vaug = [Res(f"vaug{i}") for i in range(5)]
        R_qrT = [Res(f"qrT{g}") for g in range(2)]
        R_krT = [Res(f"krT{g}") for g in range(2)]
        R_big = Res("big")
        R_vr = [[Res(f"vr{c}_{g}") for g in range(4)] for c in range(TCH)]
        R_krTM = [[Res(f"krTM{c}_{g}") for g in range(2)] for c in range(TCH)]
        R_bigall = [r for row in R_vr for r in row] + [r for row in R_krTM for r in row]
        R_attnT = [Res(f"attnT{c}") for c in range(TCH)]
        R_retT = Res("retT")
        R_P = [Res("P0"), Res("P1")]
        R_W = [Res(f"W{j}") for j in range(8)]
        R_Db = [Res(f"Db{j}") for j in range(8)]
        R_t32 = [Res(f"t32_{i}") for i in range(5)]
        R_t16 = [Res(f"t16_{i}") for i in range(3)]
        R_cs = [Res(f"cs{c}") for c in range(2)]
        R_const = Res("const")
        R_st = [Res(f"st{i}") for i in range(8)]
        R_pf = [Res(f"pf{i}") for i in range(NPF)]
        R_pb = [Res(f"pb{i}") for i in range(2)]

        ctr = {"pf": 0, "pb": 0, "t32": 0, "t16": 0, "st": 0}

        def bank():
            n = 6 if ctr.get("six", True) else NPF
            i = ctr["pf"] % n
            ctr["pf"] += 1
            if i >= NPF:
                return pvbanks[i - NPF], R_pv[i - NPF]
            return pf[i], R_pf[i]

        def bankb():
            i = ctr["pb"] % 2
            ctr["pb"] += 1
            return pb[i], R_pb[i]

        def tmp32():
            i = ctr["t32"] % 5
            ctr["t32"] += 1
            return t32[:, i, :], R_t32[i]

        def tmp16():
            i = ctr["t16"] % 3
            ctr["t16"] += 1
            return t16[:, i, :], R_t16[i]

        def stat(n):
            i = ctr["st"] % 8
            ctr["st"] += 1
            return st[:, i * 4:i * 4 + n], R_st[i]

        def mm(out, lhsT, rhs, start, stop, reads, writes):
            S.op("pe", lambda e: e.matmul(out, lhsT, rhs, start=start, stop=stop), reads, writes)

        def tr(out, in_, reads, writes):
            S.op("pe", lambda e: e.transpose(out, in_, cb[:, 0, :]), reads + [R_const], writes)

        def act(out, in_, func, reads, writes, scale=1.0, bias=None, accum=None):
            kw = {}
            if bias is not None:
                kw["bias"] = bias
            if accum is not None:
                kw["accum_out"] = accum
            S.op("act", lambda e: e.activation(out, in_, func, scale=scale, **kw), reads, writes)

        def tt(eng, out, in0, in1, op, reads, writes):
            S.op(eng, lambda e: e.tensor_tensor(out, in0, in1, op), reads, writes)

        def stt(eng, out, in0, scalar, in1, op0, op1, reads, writes, accum=None):
            if accum is None:
                S.op(eng, lambda e: e.scalar_tensor_tensor(out, in0, scalar, in1, op0, op1), reads, writes)
            else:
                S.op(eng, lambda e: e.scalar_tensor_tensor(out, in0, scalar, in1, op0, op1, accum_out=accum),
                     reads, writes)

        def cp(eng, out, in_, reads, writes):
            if eng == "act":
                S.op(eng, lambda e: e.activation(out, in_, AF.Identity), reads, writes)
            else:
                S.op(eng, lambda e: e.tensor_copy(out, in_), reads, writes)

        def dma(eng, out, in_, sem, reads, writes):
            S.op(eng, lambda e: e.dma_start(out=out, in_=in_), reads, writes, dsem=sem)

        sem_ring = [S.dsem(f"ring{i}") for i in range(NSLOT)]
        sem_x = [S.dsem(f"x{c}") for c in range(6)]
        sem_cs = [S.dsem(f"cs{c}") for c in range(2)]
        sem_c = S.dsem("const")
        sem_o = [S.dsem(f"o{c}") for c in range(6)]
        sem_o2 = [S.dsem(f"oh{i}") for i in range(6)]

        cf = t32[:, 0, :].rearrange("p (a b) -> p a b", a=4)
        for (dst, src) in ((cf, cf_d), (gT[:], gT_d), (qkg[:], qkg_d), (esink[:], sinks_d), (sqk[:], sqk_d)):
            dma("pool", dst, src, sem_c, [], [R_const, R_t32[0]])
        PRE_BOUNDS = [0, 3, 7, 11, 15, 23, 31, 38, NU]
        NG = len(PRE_BOUNDS) - 1
        sem_pre = [S.dsem(f"pre{g}") for g in range(NG)]
        R_scr = [Res(f"scr{g}") for g in range(NG)]
        unit_group = [next(g for g in range(NG) if PRE_BOUNDS[g] <= u < PRE_BOUNDS[g + 1]) for u in range(NU)]

        def prepass_groups(gs):
            for g in gs:
                for u in range(PRE_BOUNDS[g], PRE_BOUNDS[g + 1]):
                    ncols, pieces = units[u]
                    for (base, nk, pitch, off, width, src, row0, col0) in pieces:
                        dst = scr[u][:, base:base + nk * pitch].rearrange("p (k c) -> p k c", c=pitch)[:, :, off:off + width]
                        srcap = wsrc[src][row0:row0 + nk * 128, col0:col0 + width].rearrange("(k p) c -> p k c", p=128)
                        dma("pool", dst, srcap, sem_pre[g], [], [])
                R_scr[g].writer = ("d", sem_pre[g], sem_pre[g].count)

        S.op("dve", lambda e: e.tensor_copy(cb[:], cf), [R_const, R_t32[0]], [R_const])
        S.op("dve", lambda e: e.memset(vaug[:], 1.0), [], R_vaug)
        S.op("dve", lambda e: e.memset(kT[:], 0.0), [], [R_kTc, R_kTp])
        S.op("dve", lambda e: e.memset(cst[:, 0:1], 64.0 * EPS), [], [R_const])
        S.op("dve", lambda e: e.memset(cst[:, 1:2], EPS), [], [R_const])
        act(esink[:], esink[:], AF.Exp, [R_const], [R_const])

        total_units = ntiles * NU
        ws = {"next_load": 0, "next_use": 0}

        def ws_load():
            n = ws["next_load"]
            if n >= total_units:
                return
            ws["next_load"] += 1
            u = n % NU
            s = n % NSLOT
            ncols = units[u][0]
            assert R_scr[unit_group[u]].writer is not None
            dma("sp", ring[:, s, 0:ncols], scr[u][:, 0:ncols], sem_ring[s], [R_scr[unit_group[u]]], [R_ring[s]])

        def ws_get():
            n = ws["next_use"]
            ws["next_use"] += 1
            s = n % NSLOT
            assert n < ws["next_load"]
            return ring[:, s, :], R_ring[s]

        def ws_release():
            ws_load()

        def load_x(gc):
            if gc < nch:
                sl = gc % NXS
                dma("act", xt[:, sl, :], x_d[gc * 128:(gc + 1) * 128, :], sem_x[sl], [], [R_xt[sl]])

        def load_cs(t, i):
            if t < ntiles and i < 16:
                gc = t * TCH + (i % TCH)
                b = i % 2
                dma("act", csb[:, b, :], cs_d[gc * 128:(gc + 1) * 128, :], sem_cs[b], [], [R_cs[b]])

        def rstd_from_ssq(ssq, R_ssq, n, scale):
            l, R_l = stat(n)
            act(l, ssq, AF.Ln, [R_ssq, R_const], [R_l], scale=scale, bias=cst[:, 1:2])
            r, R_r = stat(n)
            act(r, l, AF.Exp, [R_l], [R_r], scale=-0.5)
            return r, R_r

        junkb = sb("junkb", [128, 512], BF16)
        rtmb = sb("rtmb", [128, 4, 512], BF16)
        R_rtm = [Res(f"rtm{h}") for h in range(4)]

        def norm_s1(gc):
            sl = gc % NXS
            ssq, R_s = stat(1)
            h, R_h = tmp16()
            stt("dve", h, xt[:, sl, :], 1.0, xt[:, sl, :], ALU.mult, ALU.mult, [R_xt[sl]], [R_s, R_h], accum=ssq)
            r, R_r = rstd_from_ssq(ssq, R_s, 1, 1.0 / D)
            act(h, xt[:, sl, :], AF.Identity, [R_xt[sl], R_r], [R_h], scale=r)
            return h, R_h

        def norm_s2(c, gcol, h, R_h):
            for hf in range(2):
                p, R_p = bankb()
                for k in range(4):
                    kk = 4 * hf + k
                    tr(p[:, k * 128:(k + 1) * 128], h[:, kk * 128:(kk + 1) * 128], [R_h], [R_p])
                tt("dve", hT[:, 4 * hf:4 * hf + 4, c * 128:(c + 1) * 128], p[:].rearrange("p (k t) -> p k t", k=4),
                   gT[:, gcol + 4 * hf:gcol + 4 * hf + 4].unsqueeze(2).to_broadcast([128, 4, 128]), ALU.mult,
                   [R_p, R_const], [R_hT[c]])

        def fm_group(slot, R_slot, kbase, rhs_fn, nk, reads):
            p, R_p = bank()
            for k in range(nk):
                mm(p[:], slot[:, kbase + k, :], rhs_fn(k), k == 0, k == nk - 1, [R_slot] + reads, [R_p])
            return p, R_p

        def phase_B(t):
            if t > 0:
                cp("dve", kT[:, :, :, 0:128], kT[:, :, :, 512:640], [R_kTc], [R_kTp])
                cp("dve", vaug[:, 0, :, :], vaug[:, 4, :, :], [R_vaug[4]], [R_vaug[0]])
            pend = []

            def qk_s2(p, R_p, sq, R_sq, ui, et):
                p2, R_p2 = bank()
                mm(p2[:], cb[:, 1, :], sq[:, 0:512], True, True, [R_sq, R_const], [R_p2])
                r, R_r = tmp32()
                act(r, p2[:], AF.Ln, [R_p2, R_const], [R_r], bias=cst[:, 0:1])
                act(r, r, AF.Exp, [R_r], [R_r], scale=-0.5)
                if ui < 2:
                    e = 4 * ui + et
                    stt("dve", qaT[:, e, :], p[:], qkg[:, 0:1], r, ALU.mult, ALU.mult,
                        [R_p, R_r, R_const], [R_qaT[e]])
                else:
                    for half in range(2):
                        lo, hi = half * 64, (half + 1) * 64
                        stt("dve", kT[lo:hi, half, et, 128:640], p[lo:hi, :], qkg[lo:hi, 1:2], r[lo:hi, :],
                            ALU.mult, ALU.mult, [R_p, R_r, R_const], [R_kTc])

            for ui in range(3):
                slot, R_slot = ws_get()
                s3 = slot.rearrange("p (k c) -> p k c", k=8)
                ntile = 4 if ui < 2 else 2
                for et in range(ntile):
                    p, R_p = bank()
                    for k in range(8):
                        mm(p[:], s3[:, k, et * 128:(et + 1) * 128], hT[:, k, :], k == 0, k == 7,
                           [R_slot] + R_hT, [R_p])
                    sq, R_sq = tmp16()
                    act(sq[:, 0:512], p[:], AF.Square, [R_p], [R_sq])
                    pend.append((p, R_p, sq, R_sq, ui, et))
                    if len(pend) > 1:
                        qk_s2(*pend.pop(0))
                if ui == 2:
                    for c in range(TCH):
                        p, R_p = bank()
                        for k in range(8):
                            mm(p[:, 0:128], hT[:, k, c * 128:(c + 1) * 128], s3[:, k, 256:384], k == 0, k == 7,
                               [R_slot, R_hT[c]], [R_p])
                        cp("act", vaug[:, c + 1, :, 0:64], p[:, 0:128].rearrange("p (a d) -> p a d", a=2),
                           [R_p], [R_vaug[c + 1]])
                        if pend:
                            qk_s2(*pend.pop(0))
                ws_release()
            while pend:
                qk_s2(*pend.pop(0))

        def phase_C(t):
            pend = []

            def rot_s2(rot, R_rot, which, g, c):
                pt, R_pt = bankb()
                for j in range(4):
                    tr(pt[:, j * 128:(j + 1) * 128], rot[:, j * 128:(j + 1) * 128], [R_rot], [R_pt])
                dstT = qrT if which == 0 else krT
                R_d = (R_qrT if which == 0 else R_krT)[g]
                cp("act", dstT[:, 4 * g:4 * g + 4, c * 128:(c + 1) * 128],
                   pt[:].rearrange("p (j t) -> p j t", j=4), [R_pt], [R_d])

            for which in range(2):
                for g in range(2):
                    slot, R_slot = ws_get()
                    s3 = slot.rearrange("p (k c) -> p k c", k=8)
                    for c in range(TCH):
                        it = which * 8 + g * 4 + c
                        load_cs(t, it + 1)
                        cos = csb[:, it % 2, 0:128]
                        sin = csb[:, it % 2, 128:256]
                        R_c = R_cs[it % 2]
                        p, R_p = bank()
                        for k in range(8):
                            mm(p[:], hT[:, k, c * 128:(c + 1) * 128], s3[:, k, :], k == 0, k == 7,
                               [R_slot, R_hT[c]], [R_p])
                        qs, R_qs = tmp32()
                        for hh in range(2):
                            h = 2 * g + hh
                            act(qs[:, hh * 256:(hh + 1) * 256], p[:, hh * 256:(hh + 1) * 256], AF.Identity,
                                [R_p, R_const], [R_qs], scale=sqk[:, which * 4 + h:which * 4 + h + 1])
                        A, R_A = tmp32()
                        Bt, R_B = tmp32()
                        qs4 = qs.rearrange("p (h i two) -> p h i two", h=2, two=2)
                        A4 = A.rearrange("p (h i two) -> p h i two", h=2, two=2)
                        B4 = Bt.rearrange("p (h i two) -> p h i two", h=2, two=2)
                        tt("dve", A4, qs4, cos.unsqueeze(1).unsqueeze(3).to_broadcast([128, 2, 128, 2]), ALU.mult,
                           [R_qs, R_c], [R_A])
                        sinb = sin.unsqueeze(1).to_broadcast([128, 2, 128])
                        stt("dve", B4[:, :, :, 0], qs4[:, :, :, 1], -1.0, sinb, ALU.mult, ALU.mult, [R_qs, R_c], [R_B])
                        tt("dve", B4[:, :, :, 1], qs4[:, :, :, 0], sinb, ALU.mult, [R_qs, R_c], [R_B])
                        if which == 0:
                            rot, R_rot = tmp16()
                            rot = rot[:, 0:512]
                            tt("dve", rot, A, Bt, ALU.add, [R_A, R_B], [R_rot])
                        else:
                            rot = krTM[:, c, g * 512:(g + 1) * 512]
                            R_rot = R_krTM[c][g]
                            tt("dve", rot, A, Bt, ALU.add, [R_A, R_B], [R_rot, R_big])
                        pend.append((rot, R_rot, which, g, c))
                        if len(pend) > 2:
                            rot_s2(*pend.pop(0))
                    ws_release()
            for g in range(4):
                slot, R_slot = ws_get()
                s3 = slot.rearrange("p (k c) -> p k c", k=8)
                for c in range(TCH):
                    p, R_p = bank()
                    for k in range(8):
                        mm(p[:], hT[:, k, c * 128:(c + 1) * 128], s3[:, k, :], k == 0, k == 7,
                           [R_slot, R_hT[c]], [R_p])
                    cp("act", vr[:, c, g * 512:(g + 1) * 512], p[:], [R_p], [R_vr[c][g], R_big])
                    if pend:
                        rot_s2(*pend.pop(0))
                ws_release()

        def D1(t, c):
            gc = t * TCH + c
            blks = (1,) if gc == 0 else (0, 1)
            for kv in range(2):
                for blk in blks:
                    for half in range(2):
                        p, R_p = bank()
                        lo, hi = half * 64, (half + 1) * 64
                        mm(p[:], kT[:, half, kv, (c + blk) * 128:(c + blk + 1) * 128],
                           qaT[:, 4 * kv:4 * kv + 4, c * 128:(c + 1) * 128], True, True,
                           [R_kTc, R_kTp] + R_qaT[4 * kv:4 * kv + 4], [R_p])
                        dst = Pb4[:, kv, blk, half, :]
                        act(dst, p[:], AF.Exp, [R_p], [R_P[kv]], scale=8.0)
                        tt("dve", dst.rearrange("p (t q) -> p t q", t=4), dst.rearrange("p (t q) -> p t q", t=4),
                           cb[:, 3 - blk, :].unsqueeze(1).to_broadcast([128, 4, 128]), ALU.mult,
                           [R_P[kv], R_const], [R_P[kv]])

        def D2(t, c):
            gc = t * TCH + c
            blks = (1,) if gc == 0 else (0, 1)
            pvb = [(pvbanks[i], R_pv[i]) for i in range(2)]
            for kv in range(2):
                for tq in range(4):
                    for half in range(2):
                        h = 8 * kv + 2 * tq + half
                        if h == 14:
                            pvb.append(bank())
                        bnk, R_b = pvb[h // 7]
                        o = bnk[:, (h % 7) * 65:(h % 7) * 65 + 65]
                        for bi, blk in enumerate(blks):
                            mm(o, Pb4[:, kv, blk, half, tq * 128:(tq + 1) * 128], vaug[:, c + blk, kv, :],
                               bi == 0, bi == len(blks) - 1, [R_P[kv], R_vaug[c + blk]], [R_b])
            for b in range(3):
                nh = 7 if b < 2 else 2
                bnk, R_b = pvb[b]
                b3 = bnk[:, 0:nh * 65].rearrange("p (h e) -> p h e", e=65)
                tt("dve", st_den[:, 7 * b:7 * b + nh].unsqueeze(2), b3[:, :, 64:65],
                   esink[:, 7 * b:7 * b + nh].unsqueeze(2), ALU.add, [R_b, R_const], [R_stden])
            S.op("dve", lambda e: e.reciprocal(st_rec[:], st_den[:]), [R_stden], [R_strec])
            atm, R_atm = tmp16()
            a3 = atm.rearrange("p (h d) -> p h d", d=64)
            for b in range(3):
                nh = 7 if b < 2 else 2
                bnk, R_b = pvb[b]
                b3 = bnk[:, 0:nh * 65].rearrange("p (h e) -> p h e", e=65)
                tt("dve", a3[:, 7 * b:7 * b + nh, :], b3[:, :, 0:64],
                   st_rec[:, 7 * b:7 * b + nh].unsqueeze(2).to_broadcast([128, nh, 64]), ALU.mult,
                   [R_b, R_strec], [R_atm])
            return atm, R_atm

        def D3(t, c, atm, R_atm):
            for hf in range(2):
                pt, R_pt = bankb()
                for e in range(4):
                    ee = 4 * hf + e
                    tr(pt[:, e * 128:(e + 1) * 128], atm[:, ee * 128:(ee + 1) * 128], [R_atm], [R_pt])
                cp("dve", attnT[:, 4 * hf:4 * hf + 4, c * 128:(c + 1) * 128], pt[:].rearrange("p (e t) -> p e t", e=4),
                   [R_pt], [R_attnT[c]])

        def E1(t, c):
            cs_ = slice(c * 128, (c + 1) * 128)
            pin, R_pin = bank()
            for h in range(4):
                for dt in range(2):
                    j = 2 * h + dt
                    mm(pin[:, h * 128:(h + 1) * 128], krT[:, j, cs_], qrT[:, j, cs_], dt == 0, dt == 1,
                       [R_krT[h // 2], R_qrT[h // 2]], [R_pin])
            tt("dve", innb[:].rearrange("p (h i) -> p h i", h=4), pin[:].rearrange("p (h i) -> p h i", h=4),
               cb[:, 2, :].unsqueeze(1).to_broadcast([128, 4, 128]), ALU.mult, [R_pin, R_const], [R_inn])

        def E2(t, c):
            gc = t * TCH + c
            cs_ = slice(c * 128, (c + 1) * 128)
            inn = innb[:]
            for h in range(4):
                po, R_po = bank()
                mm(po[:], inn[:, h * 128:(h + 1) * 128], vr[:, c, h * 512:(h + 1) * 512], True, gc == 0,
                   [R_inn, R_vr[c][h]], [R_po])
                if gc > 0:
                    for dt in range(2):
                        j = 2 * h + dt
                        mm(po[:], qrT[:, j, cs_], Db[:, j, :], False, dt == 1, [R_qrT[h // 2], R_Db[j]], [R_po])
                ssq, R_s = stat(1)
                act(junkb[:], po[:], AF.Square, [R_po], [R_s], accum=ssq)
                r, R_r = rstd_from_ssq(ssq, R_s, 1, 1.0 / 512)
                S.op("dve", lambda e, h=h, po=po, r=r: e.tensor_scalar(rtmb[:, h, :], po[:], r, None, ALU.mult),
                     [R_po, R_r], [R_rtm[h]])

        def E3(t, c):
            gc = t * TCH + c
            if gc >= nch - 1:
                return
            for h in range(4):
                g128 = GAMMA[h] ** 128
                for dt in range(2):
                    j = 2 * h + dt
                    pu, R_pu = bank()
                    mm(pu[:], krTM[:, c, h * 256 + dt * 128:h * 256 + (dt + 1) * 128],
                       vr[:, c, h * 512:(h + 1) * 512], True, True, [R_vr[c][h], R_krTM[c][h // 2]], [R_pu])
                    if gc == 0:
                        cp("dve", Wst[:, j, :], pu[:], [R_pu], [R_W[j]])
                    else:
                        stt("dve", Wst[:, j, :], Wst[:, j, :], g128, pu[:], ALU.mult, ALU.add,
                            [R_pu, R_W[j]], [R_W[j]])
                    act(Db[:, j, :], Wst[:, j, :], AF.Identity, [R_W[j]], [R_Db[j]], scale=g128)

        def E4(t, c):
            cs_ = slice(c * 128, (c + 1) * 128)
            for h in range(4):
                pt, R_pt = bankb()
                for j in range(4):
                    tr(pt[:, j * 128:(j + 1) * 128], rtmb[:, h, j * 128:(j + 1) * 128], [R_rtm[h]], [R_pt])
                cp("dve", retT[:, 4 * h:4 * h + 4, cs_], pt[:].rearrange("p (j t) -> p j t", j=4),
                   [R_pt], [R_retT])

        def phase_DE(t):
            ctr["six"] = False
            D1(t, 0)
            E1(t, 0)
            for c in range(TCH):
                atm, R_atm = D2(t, c)
                E2(t, c)
                E3(t, c)
                if c + 1 < TCH:
                    D1(t, c + 1)
                    E1(t, c + 1)
                D3(t, c, atm, R_atm)
                E4(t, c)
            ctr["six"] = True

        def phase_F(t):
            for u in range(4):
                slot, R_slot = ws_get()
                s3 = slot.rearrange("p (k c) -> p k c", k=8)
                for et in range(4):
                    e = 4 * u + et
                    p, R_p = bank()
                    for k in range(8):
                        mm(p[:], s3[:, k, et * 128:(et + 1) * 128], hT[:, k, :], k == 0, k == 7,
                           [R_slot] + R_hT, [R_p])
                    gs, R_gs = tmp32()
                    act(gs, p[:], AF.Silu, [R_p], [R_gs])
                    tt("dve", retT[:, e, :], retT[:, e, :], gs, ALU.mult, [R_retT, R_gs], [R_retT])
                ws_release()
            tap("retTg", retT[:], [128, 16, 512], BF16, [R_retT])
            for m in range(8):
                ua, R_ua = ws_get()
                ub, R_ub = ws_get()
                a3 = ua[:, 0:3072].rearrange("p (k c) -> p k c", c=128)
                b3 = ub[:, 0:2048].rearrange("p (k c) -> p k c", c=128)
                pza, R_pza = fm_group(a3, R_ua, 0, lambda k: hT[:, k, :], 8, R_hT)
                ta, R_ta = tmp32()
                act(ta, pza[:], AF.Tanh, [R_pza], [R_ta], scale=0.5)
                pba, R_pba = fm_group(a3, R_ua, 8, lambda k: attnT[:, k, :], 8, R_attnT)
                A, R_A = tmp32()
                stt("dve", A, ta, 1.0, pba[:], ALU.add, ALU.mult, [R_ta, R_pba], [R_A])
                pzr, R_pzr = fm_group(a3, R_ua, 16, lambda k: hT[:, k, :], 8, R_hT)
                trr, R_tr = tmp32()
                act(trr, pzr[:], AF.Tanh, [R_pzr], [R_tr], scale=0.5)
                pbr, R_pbr = fm_group(b3, R_ub, 0, lambda k: retT[:, k, :], 16, [R_retT])
                Bt, R_B = tmp32()
                stt("dve", Bt, trr, 1.0, pbr[:], ALU.add, ALU.mult, [R_tr, R_pbr], [R_B])
                tt("dve", qaT[:, m, :], A, Bt, ALU.add, [R_A, R_B], [R_qaT[m]])
                ws_release()
                ws_release()
            tap("merged", qaT[:], [128, 8, 512], BF16, R_qaT)
            pend = []
            for hf in range(2):
                slot, R_slot = ws_get()
                s3 = slot.rearrange("p (k c) -> p k c", k=8)
                for c in range(TCH):
                    p, R_p = bank()
                    for m in range(8):
                        mm(p[:], qaT[:, m, c * 128:(c + 1) * 128], s3[:, m, :], m == 0, m == 7,
                           [R_slot, R_qaT[m]], [R_p])
                    sl = (t * TCH + c) % NXS
                    xs = xt[:, sl, hf * 512:(hf + 1) * 512]
                    stt("dve", xs, p[:], 0.5, xs, ALU.mult, ALU.add, [R_p, R_xt[sl]], [R_xt[sl]])
                    if hf == 1:
                        h, R_h = norm_s1(t * TCH + c)
                        pend.append((c, 8, h, R_h))
                        if len(pend) > 1:
                            norm_s2(*pend.pop(0))
                ws_release()
            while pend:
                norm_s2(*pend.pop(0))
            tap("x2", xt[:, 0:TCH, :], [128, TCH, D], F32, R_xt)
            tap("h2T", hT[:], [128, 8, 512], BF16, R_hT)

        def phase_G(t):
            for u in range(11):
                slot, R_slot = ws_get()
                s3 = slot.rearrange("p (k c) -> p k c", c=128)
                for ff in range(2):
                    f = 2 * u + ff
                    pg, R_pg = fm_group(s3, R_slot, ff * 16, lambda k: hT[:, k, :], 8, R_hT)
                    pu, R_pu = fm_group(s3, R_slot, ff * 16 + 8, lambda k: hT[:, k, :], 8, R_hT)
                    sg, R_sg = tmp32()
                    act(sg, pg[:], AF.Silu, [R_pg], [R_sg])
                    tt("dve", actT[:, f, :], sg, pu[:], ALU.mult, [R_sg, R_pu], [R_big] + R_bigall)
                ws_release()
            tap("actT", big[:], [128, 12288], BF16, [R_big])
            pend = []
            for hf in range(2):
                slots = [ws_get() for _ in range(3)]
                for c in range(TCH):
                    gc = t * TCH + c
                    if hf == 1 and t + 1 < ntiles:
                        h, R_h = norm_s1(gc + TCH)
                        pend.append((c, 0, h, R_h))
                    p, R_p = bank()
                    for f in range(22):
                        sl, R_sl = slots[f // 8]
                        s3 = sl.rearrange("p (k c) -> p k c", c=512)
                        mm(p[:], actT[:, f, c * 128:(c + 1) * 128], s3[:, f % 8, :], f == 0, f == 21,
                           [R_sl, R_big], [R_p])
                    rows = slice(gc * 128, (gc + 1) * 128)
                    sl = gc % NXS
                    xs = xt[:, sl, hf * 512:(hf + 1) * 512]
                    if hf == 0:
                        tt("dve", xs, xs, p[:], ALU.add, [R_p, R_xt[sl]], [R_xt[sl]])
                        dma("pool", out_d[rows, 0:512], xs, sem_o[sl], [R_xt[sl]], [])
                    else:
                        ti = ctr["t32"] % 5
                        sg_, R_sg = tmp32()
                        tt("dve", sg_, xs, p[:], ALU.add, [R_p, R_xt[sl]], [R_sg])
                        dma("pool", out_d[rows, 512:1024], sg_, sem_o2[ti], [R_sg], [])
                        load_x(gc + NXS)
                        if len(pend) > 1:
                            norm_s2(*pend.pop(0))
                for _ in range(3):
                    ws_release()
            while pend:
                norm_s2(*pend.pop(0))

        taps = {}

        def tap(name, ap, shape, dt, reads):
            if not debug or name in taps:
                return
            d = nc.dram_tensor("dbg_" + name, list(shape), dt, kind="ExternalOutput").ap()
            taps[name] = d
            dma("pool", d, ap, S.dsem("dbg_" + name), reads, [])

        for gc in range(NXS):
            load_x(gc)
        prepass_groups([0, 1, 2])
        for _ in range(NSLOT):
            ws_load()
        for c in range(TCH):
            h, R_h = norm_s1(c)
            norm_s2(c, 0, h, R_h)
        for t in range(ntiles):
            tap("hT", hT[:], [128, 8, 512], BF16, R_hT)
            load_cs(t, 0)
            phase_B(t)
            if t == 0:
                prepass_groups([3, 4])
            tap("qaT", qaT[:], [128, 8, 512], BF16, R_qaT)
            tap("kT", kT[:, 0, :, :], [128, 2, 640], BF16, [R_kTc, R_kTp])
            tap("vaug", vaug[:], [128, 5, 2, 65], BF16, R_vaug)
            phase_C(t)
            if t == 0:
                prepass_groups([5, 6, 7])
            tap("qrT", qrT[:], [128, 8, 512], BF16, R_qrT)
            tap("krT", krT[:], [128, 8, 512], BF16, R_krT)
            tap("big", big[:], [128, 12288], BF16, [R_big] + R_bigall)
            phase_DE(t)
            tap("attnT", attnT[:], [128, 8, 512], BF16, R_attnT)
            tap("retT", retT[:], [128, 16, 512], BF16, [R_retT])
            phase_F(t)
            phase_G(t)

        S.finalize()

        with contextlib.ExitStack() as es2:
            esems = {e: es2.enter_context(nc.semaphore(f"sem_{e}")) for e in Sched.ENGS}
            for dsm in S.dsems:
                dsm.handle = es2.enter_context(nc.semaphore(f"dsem_{dsm.name}"))
            block = es2.enter_context(nc.Block())

            @block.tensor
            def _(e):
                S.emit_engine("pe", e, esems)

            @block.scalar
            def _(e):
                S.emit_engine("act", e, esems)

            @block.vector
            def _(e):
                S.emit_engine("dve", e, esems)

            @block.gpsimd
            def _(e):
                S.emit_engine("pool", e, esems)
                for dsm in S.dsems:
                    if dsm.count and (dsm.name.startswith("o") or dsm.name.startswith("dbg_")):
                        e.wait_ge(dsm.handle, dsm.count)

            @block.sync
            def _(e):
                S.emit_engine("sp", e, esems)
    return nc


def host_consts(ntok):
    f32 = np.float32
    pos = np.arange(ntok, dtype=f32)
    theta = (1.0 / (np.float32(10000.0) ** np.linspace(0.0, 1.0, 128, dtype=f32))).astype(f32)
    ang = (pos[:, None] * theta[None, :]).astype(f32).astype(np.float64)
    cos = np.cos(ang).astype(f32)
    sin = np.sin(ang).astype(f32)
    cs = np.concatenate([cos, sin], axis=1).astype(f32)
    i = np.arange(128, dtype=np.float64)
    sqk = np.empty((128, 8), f32)
    for h in range(4):
        g = 1.0 - 2.0 ** (-5.0 - h)
        sqk[:, h] = g ** i
        sqk[:, 4 + h] = g ** (-i) / 16.0
    cf = np.zeros((128, 4, 128), f32)
    cf[:, 0, :] = np.eye(128, dtype=f32)
    bd = np.zeros((128, 128), f32)
    bd[:64, :64] = 1.0
    bd[64:, 64:] = 1.0
    cf[:, 1, :] = bd
    s = np.arange(128)[:, None]
    q = np.arange(128)[None, :]
    cf[:, 2, :] = (s <= q).astype(f32)
    cf[:, 3, :] = (s > q).astype(f32)
    return cs, sqk, cf


def make_in_maps(inputs, ntiles, ncores):
    ntok = ntiles * TCH * 128
    f = lambda a: np.ascontiguousarray(np.asarray(a, dtype=np.float32))
    cs, sqk, cf = host_consts(ntok)
    gT = np.concatenate([f(inputs["norm_mix_gain"])[0].reshape(8, 128).T,
                         f(inputs["norm_ffn_gain"])[0].reshape(8, 128).T], axis=1)
    qkg = np.stack([np.tile(f(inputs["q_norm_gain"])[0], 2), np.tile(f(inputs["k_norm_gain"])[0], 2)], axis=1)
    sinks = np.tile(f(inputs["attn_sinks"])[0][None, :], (128, 1))
    shared = {
        "w_in": f(inputs["w_in"])[0], "w_ba": f(inputs["w_branch_attn"])[0], "w_br": f(inputs["w_branch_ret"])[0],
        "w_o": f(inputs["w_out"])[0], "w_g": f(inputs["w_ffn_gate"])[0], "w_u": f(inputs["w_ffn_up"])[0],
        "w_d": f(inputs["w_ffn_down"])[0],
        "gT": f(gT), "qkg": f(qkg), "sinks": f(sinks), "cs": cs, "sqk": sqk, "cf": cf,
    }
    x = f(inputs["x"])
    maps = []
    for b in range(ncores):
        m = dict(shared)
        m["x"] = np.ascontiguousarray(x[b, :ntok, :])
        maps.append(m)
    return maps


def run(inputs, ntiles=16, ncores=NCORES):
    nc = build(ntiles)
    maps = make_in_maps(inputs, ntiles, ncores)
    res = run_bass_kernel_spmd(nc, maps, core_ids=list(range(ncores)))
    return np.stack([np.asarray(r["out"]) for r in res.results], axis=0)


def kernel(**inputs):
    out = run(inputs, 16, NCORES)
    return out.astype(np.float32)
```

```python
import numpy as np
import contextlib
import concourse.bass as bass
import concourse.mybir as mybir
from concourse.bass_utils import run_bass_kernel_spmd

F32 = mybir.dt.float32
BF16 = mybir.dt.bfloat16
AF = mybir.ActivationFunctionType
ALU = mybir.AluOpType

D = 1024
SEQ = 8192
NCORES = 8
DFF = 2816
DIN = 9472
EPS = 1e-6
TCH = 4
NSLOT = 5
SLOTW = 4096
RET_H = 4
GAMMA = [1.0 - 2.0 ** (-5.0 - h) for h in range(RET_H)]

O_QA, O_KA, O_VA, O_QR, O_KR, O_VR, O_GR, O_ZA, O_ZR = 0, 1024, 1152, 1280, 2304, 3328, 5376, 7424, 8448


class Res:
    __slots__ = ("name", "writer", "readers")

    def __init__(self, name):
        self.name = name
        self.writer = None
        self.readers = []


class DSem:
    def __init__(self, name):
        self.name = name
        self.count = 0
        self.handle = None


class Op:
    __slots__ = ("eng", "fn", "deps", "idx", "dsem", "dval", "needed")


class Sched:
    ENGS = ("pe", "act", "dve", "pool", "sp")

    def __init__(self):
        self.streams = {e: [] for e in self.ENGS}
        self.dsems = []

    def dsem(self, name):
        s = DSem(name)
        self.dsems.append(s)
        return s

    def op(self, eng, fn, reads=(), writes=(), dsem=None):
        o = Op()
        o.eng = eng
        o.fn = fn
        o.idx = len(self.streams[eng])
        o.needed = False
        deps = []
        for r in reads:
            if r.writer is not None:
                deps.append(r.writer)
        for w in writes:
            if w.writer is not None:
                deps.append(w.writer)
            deps.extend(w.readers)
        o.deps = deps
        if dsem is not None:
            dsem.count += 16
            o.dsem = dsem
            o.dval = dsem.count
            tok = ("d", dsem, dsem.count)
        else:
            o.dsem = None
            o.dval = 0
            tok = ("e", eng, o.idx, o)
        for w in writes:
            w.writer = tok
            w.readers = []
        for r in reads:
            if dsem is None:
                r.readers = [x for x in r.readers if not (x[0] == "e" and x[1] == eng)]
            r.readers.append(tok)
        self.streams[eng].append(o)
        return o

    def finalize(self):
        for e in self.ENGS:
            for o in self.streams[e]:
                for d in o.deps:
                    if d[0] == "e":
                        if d[1] == "pe" and o.eng == "pe":
                            continue
                        d[3].needed = True
        self.cum = {}
        for e in self.ENGS:
            c = 0
            arr = []
            for o in self.streams[e]:
                if o.needed:
                    c += 1
                arr.append(c)
            self.cum[e] = arr

    def emit_engine(self, eng, engine_obj, esems):
        waited = {}
        for o in self.streams[eng]:
            need = {}
            for d in o.deps:
                if d[0] == "e":
                    if d[1] == "pe" and eng == "pe":
                        continue
                    key = ("e", d[1])
                    val = self.cum[d[1]][d[2]]
                else:
                    key = ("d", d[1])
                    val = d[2]
                if val > need.get(key, 0):
                    need[key] = val
            todo = []
            for key, val in need.items():
                if waited.get(key, 0) >= val:
                    continue
                waited[key] = val
                sem = esems[key[1]] if key[0] == "e" else key[1].handle
                todo.append((sem, val))
            for sem, val in todo[:-1]:
                engine_obj.wait_ge(sem, val)
            ins = o.fn(engine_obj)
            if todo:
                ins._wait_ge(todo[-1][0], todo[-1][1])
            if o.dsem is not None:
                ins.then_inc(o.dsem.handle, 16)
            elif o.needed:
                ins.then_inc(esems[eng], 1)


def make_units():
    units = []

    def full(src, row0, nk, col0, width=512):
        return (nk * width, [(0, nk, width, 0, width, src, row0, col0)])

    units.append(full("w_in", 0, 8, O_QA))
    units.append(full("w_in", 0, 8, O_QA + 512))
    pieces = []
    for j, c0 in enumerate((O_KA, O_KA, O_KA + 64, O_KA + 64)):
        pieces.append((0, 8, 512, j * 64, 64, "w_in", 0, c0))
    pieces.append((0, 8, 512, 256, 128, "w_in", 0, O_VA))
    pieces.append((0, 8, 512, 384, 128, "w_in", 0, O_VA))
    units.append((8 * 512, pieces))
    for g in range(2):
        units.append(full("w_in", 0, 8, O_QR + 512 * g))
    for g in range(2):
        units.append(full("w_in", 0, 8, O_KR + 512 * g))
    for g in range(4):
        units.append(full("w_in", 0, 8, O_VR + 512 * g))
    for g in range(4):
        units.append(full("w_in", 0, 8, O_GR + 512 * g))
    for m in range(8):
        units.append((24 * 128, [
            (0, 8, 128, 0, 128, "w_in", 0, O_ZA + 128 * m),
            (1024, 8, 128, 0, 128, "w_ba", 0, 128 * m),
            (2048, 8, 128, 0, 128, "w_in", 0, O_ZR + 128 * m),
        ]))
        units.append((16 * 128, [(0, 16, 128, 0, 128, "w_br", 0, 128 * m)]))
    for hf in range(2):
        units.append(full("w_o", 0, 8, 512 * hf))
    for u in range(11):
        pieces = []
        for ff in range(2):
            f = 2 * u + ff
            pieces.append((ff * 2048, 8, 128, 0, 128, "w_g", 0, 128 * f))
            pieces.append((ff * 2048 + 1024, 8, 128, 0, 128, "w_u", 0, 128 * f))
        units.append((4096, pieces))
    for hf in range(2):
        for fg in range(3):
            nk = 8 if fg < 2 else 6
            units.append(full("w_d", fg * 1024, nk, 512 * hf))
    return units


def build(ntiles=16, debug=False):
    nc = bass.Bass("TRN2", target_bir_lowering=False)
    nch = ntiles * TCH
    ntok = nch * 128

    def din(name, shape, dt=F32):
        return nc.dram_tensor(name, list(shape), dt, kind="ExternalInput").ap()

    x_d = din("x", [ntok, D])
    wsrc = {
        "w_in": din("w_in", [D, DIN]),
        "w_ba": din("w_ba", [D, D]),
        "w_br": din("w_br", [2 * D, D]),
        "w_o": din("w_o", [D, D]),
        "w_g": din("w_g", [D, DFF]),
        "w_u": din("w_u", [D, DFF]),
        "w_d": din("w_d", [DFF, D]),
    }
    gT_d = din("gT", [128, 16])
    qkg_d = din("qkg", [128, 2])
    sinks_d = din("sinks", [128, 16])
    cs_d = din("cs", [ntok, 256])
    sqk_d = din("sqk", [128, 8])
    cf_d = din("cf", [128, 4, 128])
    out_d = nc.dram_tensor("out", [ntok, D], F32, kind="ExternalOutput").ap()

    units = make_units()
    NU = len(units)
    scr = nc.dram_tensor("wscr", [NU, 128, SLOTW], BF16, kind="Internal").ap()

    S = Sched()
    es = contextlib.ExitStack()

    def sb(name, shape, dt):
        return es.enter_context(nc.sbuf_tensor(name, list(shape), dt))

    def ps(name, shape, dt):
        return es.enter_context(nc.psum_tensor(name, list(shape), dt))

    with es:
        ring = sb("ring", [128, NSLOT, SLOTW], BF16)
        NXS = 6
        xt = sb("xt", [128, NXS, D], F32)
        hT = sb("hT", [128, 8, 512], BF16)
        qaT = sb("qaT", [128, 8, 512], BF16)
        kT = sb("kT", [128, 2, 2, 640], BF16)
        vaug = sb("vaug", [128, 5, 2, 65], BF16)
        qrT = sb("qrT", [128, 8, 512], BF16)
        krT = sb("krT", [128, 8, 512], BF16)
        big = sb("big", [128, 12288], BF16)
        attnT = sb("attnT", [128, 8, 512], BF16)
        retT = sb("retT", [128, 16, 512], BF16)
        Pb4 = sb("Pb4", [128, 2, 2, 2, 512], BF16)
        Wst = sb("Wst", [128, 8, 512], F32)
        Db = sb("Db", [128, 8, 512], BF16)
        t32 = sb("t32", [128, 5, 512], F32)
        t16 = sb("t16", [128, 3, 1024], BF16)
        csb = sb("csb", [128, 2, 256], F32)
        cb = sb("cb", [128, 4, 128], BF16)
        gT = sb("gT_sb", [128, 16], F32)
        qkg = sb("qkg_sb", [128, 2], F32)
        esink = sb("esink", [128, 16], F32)
        sqk = sb("sqk_sb", [128, 8], F32)
        cst = sb("cst", [128, 4], F32)
        st = sb("st", [128, 32], F32)
        st2 = sb("st2", [128, 32], F32)
        innb = sb("innb", [128, 512], BF16)
        R_inn = Res("inn")
        st_den = st2[:, 0:16]
        st_rec = st2[:, 16:32]
        R_stden, R_strec = Res("stden"), Res("strec")

        vr = big[:, 0:8192].rearrange("p (c f) -> p c f", c=4)
        krTM = big[:, 8192:12288].rearrange("p (c f) -> p c f", c=4)
        actT = big[:, 0:22 * 512].rearrange("p (f t) -> p f t", f=22)

        NPF = 4
        pf = [ps(f"pf{i}", [128, 512], F32) for i in range(NPF)]
        pvA = ps("pvA", [128, 512], F32)
        pvB = ps("pvB", [128, 512], F32)
        pb = [ps(f"pb{i}", [128, 512], BF16) for i in range(2)]
        pvbanks = [pvA, pvB]
        R_pv = [Res("pvA"), Res("pvB")]

        R_ring = [Res(f"ring{i}") for i in range(NSLOT)]
        R_xt = [Res(f"xt{c}") for c in range(6)]
        R_hT = [Res(f"hT{c}") for c in range(TCH)]
        R_qaT = [Res(f"qaT{e}") for e in range(8)]
        R_kTc, R_kTp = Res("kTc"), Res("kTp")
        R_vaug = [Res(f"vaug{i}") for i in range(5)]
        R_qrT = [Res(f"qrT{g}") for g in range(2)]
        R_krT = [Res(f"krT{g}") for g in range(2)]
        R_big = Res("big")
        R_vr = [[Res(f"vr{c}_{g}") for g in range(4)] for c in range(TCH)]
        R_krTM = [[Res(f"krTM{c}_{g}") for g in range(2)] for c in range(TCH)]
        R_bigall = [r for row in R_vr for r in row] + [r for row in R_krTM for r in row]
        R_attnT = [Res(f"attnT{c}") for c in range(TCH)]
        R_retT = Res("retT")
        R_P = [Res("P0"), Res("P1")]
        R_W = [Res(f"W{j}") for j in range(8)]
        R_Db = [Res(f"Db{j}") for j in range(8)]
        R_t32 = [Res(f"t32_{i}") for i in range(5)]
        R_t16 = [Res(f"t16_{i}") for i in range(3)]
        R_cs = [Res(f"cs{c}") for c in range(2)]
        R_const = Res("const")
        R_st = [Res(f"st{i}") for i in range(8)]
        R_pf = [Res(f"pf{i}") for i in range(NPF)]
        R_pb = [Res(f"pb{i}") for i in range(2)]

        ctr = {"pf": 0, "pb": 0, "t32": 0, "t16": 0, "st": 0}

        def bank():
            n = 6 if ctr.get("six", True) else NPF
            i = ctr["pf"] % n
            ctr["pf"] += 1
            if i >= NPF:
                return pvbanks[i - NPF], R_pv[i - NPF]
            return pf[i], R_pf[i]

        def bankb():
            i = ctr["pb"] % 2
            ctr["pb"] += 1
            return pb[i], R_pb[i]

        def tmp32():
            i = ctr["t32"] % 5
            ctr["t32"] += 1
            return t32[:, i, :], R_t32[i]

        def tmp16():
            i = ctr["t16"] % 3
            ctr["t16"] += 1
            return t16[:, i, :], R_t16[i]

        def stat(n):
            i = ctr["st"] % 8
            ctr["st"] += 1
            return st[:, i * 4:i * 4 + n], R_st[i]

        def mm(out, lhsT, rhs, start, stop, reads, writes):
            S.op("pe", lambda e: e.matmul(out, lhsT, rhs, start=start, stop=stop), reads, writes)

        def tr(out, in_, reads, writes):
            S.op("pe", lambda e: e.transpose(out, in_, cb[:, 0, :]), reads + [R_const], writes)

        def act(out, in_, func, reads, writes, scale=1.0, bias=None, accum=None):
            kw = {}
            if bias is not None:
                kw["bias"] = bias
            if accum is not None:
                kw["accum_out"] = accum
            S.op("act", lambda e: e.activation(out, in_, func, scale=scale, **kw), reads, writes)

        def tt(eng, out, in0, in1, op, reads, writes):
            S.op(eng, lambda e: e.tensor_tensor(out, in0, in1, op), reads, writes)

        def stt(eng, out, in0, scalar, in1, op0, op1, reads, writes, accum=None):
            if accum is None:
                S.op(eng, lambda e: e.scalar_tensor_tensor(out, in0, scalar, in1, op0, op1), reads, writes)
            else:
                S.op(eng, lambda e: e.scalar_tensor_tensor(out, in0, scalar, in1, op0, op1, accum_out=accum),
                     reads, writes)

        def cp(eng, out, in_, reads, writes):
            if eng == "act":
                S.op(eng, lambda e: e.activation(out, in_, AF.Identity), reads, writes)
            else:
                S.op(eng, lambda e: e.tensor_copy(out, in_), reads, writes)

        def dma(eng, out, in_, sem, reads, writes):
            S.op(eng, lambda e: e.dma_start(out=out, in_=in_), reads, writes, dsem=sem)

        sem_ring = [S.dsem(f"ring{i}") for i in range(NSLOT)]
        sem_x = [S.dsem(f"x{c}") for c in range(6)]
        sem_cs = [S.dsem(f"cs{c}") for c in range(2)]
        sem_c = S.dsem("const")
        sem_o = [S.dsem(f"o{c}") for c in range(6)]
        sem_o2 = [S.dsem(f"oh{i}") for i in range(6)]

        cf = t32[:, 0, :].rearrange("p (a b) -> p a b", a=4)
        for (dst, src) in ((cf, cf_d), (gT[:], gT_d), (qkg[:], qkg_d), (esink[:], sinks_d), (sqk[:], sqk_d)):
            dma("pool", dst, src, sem_c, [], [R_const, R_t32[0]])
        PRE_BOUNDS = [0, 3, 7, 11, 15, 23, 31, 38, NU]
        NG = len(PRE_BOUNDS) - 1
        sem_pre = [S.dsem(f"pre{g}") for g in range(NG)]
        R_scr = [Res(f"scr{g}") for g in range(NG)]
        unit_group = [next(g for g in range(NG) if PRE_BOUNDS[g] <= u < PRE_BOUNDS[g + 1]) for u in range(NU)]

        def prepass_groups(gs):
            for g in gs:
                for u in range(PRE_BOUNDS[g], PRE_BOUNDS[g + 1]):
                    ncols, pieces = units[u]
                    for (base, nk, pitch, off, width, src, row0, col0) in pieces:
                        dst = scr[u][:, base:base + nk * pitch].rearrange("p (k c) -> p k c", c=pitch)[:, :, off:off + width]
                        srcap = wsrc[src][row0:row0 + nk * 128, col0:col0 + width].rearrange("(k p) c -> p k c", p=128)
                        dma("pool", dst, srcap, sem_pre[g], [], [])
                R_scr[g].writer = ("d", sem_pre[g], sem_pre[g].count)

        S.op("dve", lambda e: e.tensor_copy(cb[:], cf), [R_const, R_t32[0]], [R_const])
        S.op("dve", lambda e: e.memset(vaug[:], 1.0), [], R_vaug)
        S.op("dve", lambda e: e.memset(kT[:], 0.0), [], [R_kTc, R_kTp])
        S.op("dve", lambda e: e.memset(cst[:, 0:1], 64.0 * EPS), [], [R_const])
        S.op("dve", lambda e: e.memset(cst[:, 1:2], EPS), [], [R_const])
        act(esink[:], esink[:], AF.Exp, [R_const], [R_const])

        total_units = ntiles * NU
        ws = {"next_load": 0, "next_use": 0}

        def ws_load():
            n = ws["next_load"]
            if n >= total_units:
                return
            ws["next_load"] += 1
            u = n % NU
            s = n % NSLOT
            ncols = units[u][0]
            assert R_scr[unit_group[u]].writer is not None
            dma("sp", ring[:, s, 0:ncols], scr[u][:, 0:ncols], sem_ring[s], [R_scr[unit_group[u]]], [R_ring[s]])

        def ws_get():
            n = ws["next_use"]
            ws["next_use"] += 1
            s = n % NSLOT
            assert n < ws["next_load"]
            return ring[:, s, :], R_ring[s]

        def ws_release():
            ws_load()

        def load_x(gc):
            if gc < nch:
                sl = gc % NXS
                dma("act", xt[:, sl, :], x_d[gc * 128:(gc + 1) * 128, :], sem_x[sl], [], [R_xt[sl]])

        def load_cs(t, i):
            if t < ntiles and i < 16:
                gc = t * TCH + (i % TCH)
                b = i % 2
                q = "act" if t == 0 else "pool"
                dma(q, csb[:, b, :], cs_d[gc * 128:(gc + 1) * 128, :], sem_cs[b], [], [R_cs[b]])

        def rstd_from_ssq(ssq, R_ssq, n, scale):
            l, R_l = stat(n)
            act(l, ssq, AF.Ln, [R_ssq, R_const], [R_l], scale=scale, bias=cst[:, 1:2])
            r, R_r = stat(n)
            act(r, l, AF.Exp, [R_l], [R_r], scale=-0.5)
            return r, R_r

        junkb = sb("junkb", [128, 512], BF16)
        rtmb = sb("rtmb", [128, 4, 512], BF16)
        R_rtm = [Res(f"rtm{h}") for h in range(4)]

        def norm_s1(gc):
            sl = gc % NXS
            ssq, R_s = stat(1)
            h, R_h = tmp16()
            stt("dve", h, xt[:, sl, :], 1.0, xt[:, sl, :], ALU.mult, ALU.mult, [R_xt[sl]], [R_s, R_h], accum=ssq)
            r, R_r = rstd_from_ssq(ssq, R_s, 1, 1.0 / D)
            act(h, xt[:, sl, :], AF.Identity, [R_xt[sl], R_r], [R_h], scale=r)
            return h, R_h

        def norm_s2(c, gcol, h, R_h):
            for hf in range(2):
                p, R_p = bankb()
                for k in range(4):
                    kk = 4 * hf + k
                    tr(p[:, k * 128:(k + 1) * 128], h[:, kk * 128:(kk + 1) * 128], [R_h], [R_p])
                tt("dve", hT[:, 4 * hf:4 * hf + 4, c * 128:(c + 1) * 128], p[:].rearrange("p (k t) -> p k t", k=4),
                   gT[:, gcol + 4 * hf:gcol + 4 * hf + 4].unsqueeze(2).to_broadcast([128, 4, 128]), ALU.mult,
                   [R_p, R_const], [R_hT[c]])

        def fm_group(slot, R_slot, kbase, rhs_fn, nk, reads):
            p, R_p = bank()
            for k in range(nk):
                mm(p[:], slot[:, kbase + k, :], rhs_fn(k), k == 0, k == nk - 1, [R_slot] + reads, [R_p])
            return p, R_p

        def phase_B(t):
            if t > 0:
                cp("dve", kT[:, :, :, 0:128], kT[:, :, :, 512:640], [R_kTc], [R_kTp])
                cp("dve", vaug[:, 0, :, :], vaug[:, 4, :, :], [R_vaug[4]], [R_vaug[0]])
            pend = []

            def qk_s2(p, R_p, sq, R_sq, ui, et):
                p2, R_p2 = bank()
                mm(p2[:], cb[:, 1, :], sq[:, 0:512], True, True, [R_sq, R_const], [R_p2])
                r, R_r = tmp32()
                act(r, p2[:], AF.Ln, [R_p2, R_const], [R_r], bias=cst[:, 0:1])
                act(r, r, AF.Exp, [R_r], [R_r], scale=-0.5)
                if ui < 2:
                    e = 4 * ui + et
                    stt("dve", qaT[:, e, :], p[:], qkg[:, 0:1], r, ALU.mult, ALU.mult,
                        [R_p, R_r, R_const], [R_qaT[e]])
                else:
                    for half in range(2):
                        lo, hi = half * 64, (half + 1) * 64
                        stt("dve", kT[lo:hi, half, et, 128:640], p[lo:hi, :], qkg[lo:hi, 1:2], r[lo:hi, :],
                            ALU.mult, ALU.mult, [R_p, R_r, R_const], [R_kTc])

            for ui in range(3):
                slot, R_slot = ws_get()
                s3 = slot.rearrange("p (k c) -> p k c", k=8)
                ntile = 4 if ui < 2 else 2
                for et in range(ntile):
                    p, R_p = bank()
                    for k in range(8):
                        mm(p[:], s3[:, k, et * 128:(et + 1) * 128], hT[:, k, :], k == 0, k == 7,
                           [R_slot] + R_hT, [R_p])
                    sq, R_sq = tmp16()
                    act(sq[:, 0:512], p[:], AF.Square, [R_p], [R_sq])
                    pend.append((p, R_p, sq, R_sq, ui, et))
                    if len(pend) > 1:
                        qk_s2(*pend.pop(0))
                if ui == 2:
                    for c in range(TCH):
                        p, R_p = bank()
                        for k in range(8):
                            mm(p[:, 0:128], hT[:, k, c * 128:(c + 1) * 128], s3[:, k, 256:384], k == 0, k == 7,
                               [R_slot, R_hT[c]], [R_p])
                        cp("act", vaug[:, c + 1, :, 0:64], p[:, 0:128].rearrange("p (a d) -> p a d", a=2),
                           [R_p], [R_vaug[c + 1]])
                        if pend:
                            qk_s2(*pend.pop(0))
                ws_release()
            while pend:
                qk_s2(*pend.pop(0))

        def phase_C(t):
            pend = []

            def rot_s2(rot, R_rot, which, g, c):
                pt, R_pt = bankb()
                for j in range(4):
                    tr(pt[:, j * 128:(j + 1) * 128], rot[:, j * 128:(j + 1) * 128], [R_rot], [R_pt])
                dstT = qrT if which == 0 else krT
                R_d = (R_qrT if which == 0 else R_krT)[g]
                cp("act", dstT[:, 4 * g:4 * g + 4, c * 128:(c + 1) * 128],
                   pt[:].rearrange("p (j t) -> p j t", j=4), [R_pt], [R_d])

            for which in range(2):
                for g in range(2):
                    slot, R_slot = ws_get()
                    s3 = slot.rearrange("p (k c) -> p k c", k=8)
                    for c in range(TCH):
                        it = which * 8 + g * 4 + c
                        load_cs(t, it + 1)
                        cos = csb[:, it % 2, 0:128]
                        sin = csb[:, it % 2, 128:256]
                        R_c = R_cs[it % 2]
                        p, R_p = bank()
                        for k in range(8):
                            mm(p[:], hT[:, k, c * 128:(c + 1) * 128], s3[:, k, :], k == 0, k == 7,
                               [R_slot, R_hT[c]], [R_p])
                        qs, R_qs = tmp32()
                        for hh in range(2):
                            h = 2 * g + hh
                            act(qs[:, hh * 256:(hh + 1) * 256], p[:, hh * 256:(hh + 1) * 256], AF.Identity,
                                [R_p, R_const], [R_qs], scale=sqk[:, which * 4 + h:which * 4 + h + 1])
                        A, R_A = tmp32()
                        Bt, R_B = tmp32()
                        qs4 = qs.rearrange("p (h i two) -> p h i two", h=2, two=2)
                        A4 = A.rearrange("p (h i two) -> p h i two", h=2, two=2)
                        B4 = Bt.rearrange("p (h i two) -> p h i two", h=2, two=2)
                        tt("dve", A4, qs4, cos.unsqueeze(1).unsqueeze(3).to_broadcast([128, 2, 128, 2]), ALU.mult,
                           [R_qs, R_c], [R_A])
                        sinb = sin.unsqueeze(1).to_broadcast([128, 2, 128])
                        stt("dve", B4[:, :, :, 0], qs4[:, :, :, 1], -1.0, sinb, ALU.mult, ALU.mult, [R_qs, R_c], [R_B])
                        tt("dve", B4[:, :, :, 1], qs4[:, :, :, 0], sinb, ALU.mult, [R_qs, R_c], [R_B])
                        if which == 0:
                            rot, R_rot = tmp16()
                            rot = rot[:, 0:512]
                            tt("dve", rot, A, Bt, ALU.add, [R_A, R_B], [R_rot])
                        else:
                            rot = krTM[:, c, g * 512:(g + 1) * 512]
                            R_rot = R_krTM[c][g]
                            tt("dve", rot, A, Bt, ALU.add, [R_A, R_B], [R_rot, R_big])
                        pend.append((rot, R_rot, which, g, c))
                        if len(pend) > 2:
                            rot_s2(*pend.pop(0))
                    ws_release()
            for g in range(4):
                slot, R_slot = ws_get()
                s3 = slot.rearrange("p (k c) -> p k c", k=8)
                for c in range(TCH):
                    p, R_p = bank()
                    for k in range(8):
                        mm(p[:], hT[:, k, c * 128:(c + 1) * 128], s3[:, k, :], k == 0, k == 7,
                           [R_slot, R_hT[c]], [R_p])
                    cp("act", vr[:, c, g * 512:(g + 1) * 512], p[:], [R_p], [R_vr[c][g], R_big])
                    if pend:
                        rot_s2(*pend.pop(0))
                ws_release()

        def D1(t, c):
            gc = t * TCH + c
            blks = (1,) if gc == 0 else (0, 1)
            for kv in range(2):
                for blk in blks:
                    for half in range(2):
                        p, R_p = bank()
                        lo, hi = half * 64, (half + 1) * 64
                        mm(p[:], kT[:, half, kv, (c + blk) * 128:(c + blk + 1) * 128],
                           qaT[:, 4 * kv:4 * kv + 4, c * 128:(c + 1) * 128], True, True,
                           [R_kTc, R_kTp] + R_qaT[4 * kv:4 * kv + 4], [R_p])
                        dst = Pb4[:, kv, blk, half, :]
                        act(dst, p[:], AF.Exp, [R_p], [R_P[kv]], scale=8.0)
                        tt("dve", dst.rearrange("p (t q) -> p t q", t=4), dst.rearrange("p (t q) -> p t q", t=4),
                           cb[:, 3 - blk, :].unsqueeze(1).to_broadcast([128, 4, 128]), ALU.mult,
                           [R_P[kv], R_const], [R_P[kv]])

        def D2(t, c):
            gc = t * TCH + c
            blks = (1,) if gc == 0 else (0, 1)
            pvb = [(pvbanks[i], R_pv[i]) for i in range(2)]
            for kv in range(2):
                for tq in range(4):
                    for half in range(2):
                        h = 8 * kv + 2 * tq + half
                        if h == 14:
                            pvb.append(bank())
                        bnk, R_b = pvb[h // 7]
                        o = bnk[:, (h % 7) * 65:(h % 7) * 65 + 65]
                        for bi, blk in enumerate(blks):
                            mm(o, Pb4[:, kv, blk, half, tq * 128:(tq + 1) * 128], vaug[:, c + blk, kv, :],
                               bi == 0, bi == len(blks) - 1, [R_P[kv], R_vaug[c + blk]], [R_b])
            for b in range(3):
                nh = 7 if b < 2 else 2
                bnk, R_b = pvb[b]
                b3 = bnk[:, 0:nh * 65].rearrange("p (h e) -> p h e", e=65)
                tt("dve", st_den[:, 7 * b:7 * b + nh].unsqueeze(2), b3[:, :, 64:65],
                   esink[:, 7 * b:7 * b + nh].unsqueeze(2), ALU.add, [R_b, R_const], [R_stden])
            S.op("dve", lambda e: e.reciprocal(st_rec[:], st_den[:]), [R_stden], [R_strec])
            atm, R_atm = tmp16()
            a3 = atm.rearrange("p (h d) -> p h d", d=64)
            for b in range(3):
                nh = 7 if b < 2 else 2
                bnk, R_b = pvb[b]
                b3 = bnk[:, 0:nh * 65].rearrange("p (h e) -> p h e", e=65)
                tt("dve", a3[:, 7 * b:7 * b + nh, :], b3[:, :, 0:64],
                   st_rec[:, 7 * b:7 * b + nh].unsqueeze(2).to_broadcast([128, nh, 64]), ALU.mult,
                   [R_b, R_strec], [R_atm])
            return atm, R_atm

        def D3(t, c, atm, R_atm):
            for hf in range(2):
                pt, R_pt = bankb()
                for e in range(4):
                    ee = 4 * hf + e
                    tr(pt[:, e * 128:(e + 1) * 128], atm[:, ee * 128:(ee + 1) * 128], [R_atm], [R_pt])
                cp("dve", attnT[:, 4 * hf:4 * hf + 4, c * 128:(c + 1) * 128], pt[:].rearrange("p (e t) -> p e t", e=4),
                   [R_pt], [R_attnT[c]])

        def E1(t, c):
            cs_ = slice(c * 128, (c + 1) * 128)
            pin, R_pin = bank()
            for h in range(4):
                for dt in range(2):
                    j = 2 * h + dt
                    mm(pin[:, h * 128:(h + 1) * 128], krT[:, j, cs_], qrT[:, j, cs_], dt == 0, dt == 1,
                       [R_krT[h // 2], R_qrT[h // 2]], [R_pin])
            tt("dve", innb[:].rearrange("p (h i) -> p h i", h=4), pin[:].rearrange("p (h i) -> p h i", h=4),
               cb[:, 2, :].unsqueeze(1).to_broadcast([128, 4, 128]), ALU.mult, [R_pin, R_const], [R_inn])

        def E2(t, c):
            gc = t * TCH + c
            cs_ = slice(c * 128, (c + 1) * 128)
            inn = innb[:]
            for h in range(4):
                po, R_po = bank()
                mm(po[:], inn[:, h * 128:(h + 1) * 128], vr[:, c, h * 512:(h + 1) * 512], True, gc == 0,
                   [R_inn, R_vr[c][h]], [R_po])
                if gc > 0:
                    for dt in range(2):
                        j = 2 * h + dt
                        mm(po[:], qrT[:, j, cs_], Db[:, j, :], False, dt == 1, [R_qrT[h // 2], R_Db[j]], [R_po])
                ssq, R_s = stat(1)
                act(junkb[:], po[:], AF.Square, [R_po], [R_s], accum=ssq)
                r, R_r = rstd_from_ssq(ssq, R_s, 1, 1.0 / 512)
                S.op("dve", lambda e, h=h, po=po, r=r: e.tensor_scalar(rtmb[:, h, :], po[:], r, None, ALU.mult),
                     [R_po, R_r], [R_rtm[h]])

        def E3(t, c):
            gc = t * TCH + c
            if gc >= nch - 1:
                return
            for h in range(4):
                g128 = GAMMA[h] ** 128
                for dt in range(2):
                    j = 2 * h + dt
                    pu, R_pu = bank()
                    mm(pu[:], krTM[:, c, h * 256 + dt * 128:h * 256 + (dt + 1) * 128],
                       vr[:, c, h * 512:(h + 1) * 512], True, True, [R_vr[c][h], R_krTM[c][h // 2]], [R_pu])
                    if gc == 0:
                        cp("dve", Wst[:, j, :], pu[:], [R_pu], [R_W[j]])
                    else:
                        stt("dve", Wst[:, j, :], Wst[:, j, :], g128, pu[:], ALU.mult, ALU.add,
                            [R_pu, R_W[j]], [R_W[j]])
                    act(Db[:, j, :], Wst[:, j, :], AF.Identity, [R_W[j]], [R_Db[j]], scale=g128)

        def E4(t, c):
            cs_ = slice(c * 128, (c + 1) * 128)
            for h in range(4):
                pt, R_pt = bankb()
                for j in range(4):
                    tr(pt[:, j * 128:(j + 1) * 128], rtmb[:, h, j * 128:(j + 1) * 128], [R_rtm[h]], [R_pt])
                cp("dve", retT[:, 4 * h:4 * h + 4, cs_], pt[:].rearrange("p (j t) -> p j t", j=4),
                   [R_pt], [R_retT])

        def phase_DE(t):
            ctr["six"] = False
            D1(t, 0)
            E1(t, 0)
            for c in range(TCH):
                atm, R_atm = D2(t, c)
                E2(t, c)
                E3(t, c)
                if c + 1 < TCH:
                    D1(t, c + 1)
                    E1(t, c + 1)
                D3(t, c, atm, R_atm)
                E4(t, c)
            ctr["six"] = True

        def phase_F(t):
            for u in range(4):
                slot, R_slot = ws_get()
                s3 = slot.rearrange("p (k c) -> p k c", k=8)
                for et in range(4):
                    e = 4 * u + et
                    p, R_p = bank()
                    for k in range(8):
                        mm(p[:], s3[:, k, et * 128:(et + 1) * 128], hT[:, k, :], k == 0, k == 7,
                           [R_slot] + R_hT, [R_p])
                    gs, R_gs = tmp32()
                    act(gs, p[:], AF.Silu, [R_p], [R_gs])
                    tt("dve", retT[:, e, :], retT[:, e, :], gs, ALU.mult, [R_retT, R_gs], [R_retT])
                ws_release()
            tap("retTg", retT[:], [128, 16, 512], BF16, [R_retT])
            for m in range(8):
                ua, R_ua = ws_get()
                ub, R_ub = ws_get()
                a3 = ua[:, 0:3072].rearrange("p (k c) -> p k c", c=128)
                b3 = ub[:, 0:2048].rearrange("p (k c) -> p k c", c=128)
                pza, R_pza = fm_group(a3, R_ua, 0, lambda k: hT[:, k, :], 8, R_hT)
                ta, R_ta = tmp32()
                act(ta, pza[:], AF.Tanh, [R_pza], [R_ta], scale=0.5)
                pba, R_pba = fm_group(a3, R_ua, 8, lambda k: attnT[:, k, :], 8, R_attnT)
                A, R_A = tmp32()
                stt("dve", A, ta, 1.0, pba[:], ALU.add, ALU.mult, [R_ta, R_pba], [R_A])
                pzr, R_pzr = fm_group(a3, R_ua, 16, lambda k: hT[:, k, :], 8, R_hT)
                trr, R_tr = tmp32()
                act(trr, pzr[:], AF.Tanh, [R_pzr], [R_tr], scale=0.5)
                pbr, R_pbr = fm_group(b3, R_ub, 0, lambda k: retT[:, k, :], 16, [R_retT])
                Bt, R_B = tmp32()
                stt("dve", Bt, trr, 1.0, pbr[:], ALU.add, ALU.mult, [R_tr, R_pbr], [R_B])
                tt("dve", qaT[:, m, :], A, Bt, ALU.add, [R_A, R_B], [R_qaT[m]])
                ws_release()
                ws_release()
            tap("merged", qaT[:], [128, 8, 512], BF16, R_qaT)
            pend = []
            for hf in range(2):
                slot, R_slot = ws_get()
                s3 = slot.rearrange("p (k c) -> p k c", k=8)
                for c in range(TCH):
                    p, R_p = bank()
                    for m in range(8):
                        mm(p[:], qaT[:, m, c * 128:(c + 1) * 128], s3[:, m, :], m == 0, m == 7,
                           [R_slot, R_qaT[m]], [R_p])
                    sl = (t * TCH + c) % NXS
                    xs = xt[:, sl, hf * 512:(hf + 1) * 512]
                    stt("dve", xs, p[:], 0.5, xs, ALU.mult, ALU.add, [R_p, R_xt[sl]], [R_xt[sl]])
                    if hf == 1:
                        h, R_h = norm_s1(t * TCH + c)
                        pend.append((c, 8, h, R_h))
                        if len(pend) > 1:
                            norm_s2(*pend.pop(0))
                ws_release()
            while pend:
                norm_s2(*pend.pop(0))
            tap("x2", xt[:, 0:TCH, :], [128, TCH, D], F32, R_xt)
            tap("h2T", hT[:], [128, 8, 512], BF16, R_hT)

        def phase_G(t):
            for u in range(11):
                slot, R_slot = ws_get()
                s3 = slot.rearrange("p (k c) -> p k c", c=128)
                for ff in range(2):
                    f = 2 * u + ff
                    pg, R_pg = fm_group(s3, R_slot, ff * 16, lambda k: hT[:, k, :], 8, R_hT)
                    pu, R_pu = fm_group(s3, R_slot, ff * 16 + 8, lambda k: hT[:, k, :], 8, R_hT)
                    sg, R_sg = tmp32()
                    act(sg, pg[:], AF.Silu, [R_pg], [R_sg])
                    tt("dve", actT[:, f, :], sg, pu[:], ALU.mult, [R_sg, R_pu], [R_big] + R_bigall)
                ws_release()
            tap("actT", big[:], [128, 12288], BF16, [R_big])
            pend = []
            for hf in range(2):
                slots = [ws_get() for _ in range(3)]
                for c in range(TCH):
                    gc = t * TCH + c
                    if hf == 1 and t + 1 < ntiles:
                        h, R_h = norm_s1(gc + TCH)
                        pend.append((c, 0, h, R_h))
                    p, R_p = bank()
                    for f in range(22):
                        sl, R_sl = slots[f // 8]
                        s3 = sl.rearrange("p (k c) -> p k c", c=512)
                        mm(p[:], actT[:, f, c * 128:(c + 1) * 128], s3[:, f % 8, :], f == 0, f == 21,
                           [R_sl, R_big], [R_p])
                    rows = slice(gc * 128, (gc + 1) * 128)
                    sl = gc % NXS
                    xs = xt[:, sl, hf * 512:(hf + 1) * 512]
                    if hf == 0:
                        tt("dve", xs, xs, p[:], ALU.add, [R_p, R_xt[sl]], [R_xt[sl]])
                        dma("pool", out_d[rows, 0:512], xs, sem_o[sl], [R_xt[sl]], [])
                    else:
                        ti = ctr["t32"] % 5
                        sg_, R_sg = tmp32()
                        tt("dve", sg_, xs, p[:], ALU.add, [R_p, R_xt[sl]], [R_sg])
                        dma("pool", out_d[rows, 512:1024], sg_, sem_o2[ti], [R_sg], [])
                        load_x(gc + NXS)
                        if len(pend) > 1:
                            norm_s2(*pend.pop(0))
                for _ in range(3):
                    ws_release()
            while pend:
                norm_s2(*pend.pop(0))

        taps = {}

        def tap(name, ap, shape, dt, reads):
            if not debug or name in taps:
                return
            d = nc.dram_tensor("dbg_" + name, list(shape), dt, kind="ExternalOutput").ap()
            taps[name] = d
            dma("pool", d, ap, S.dsem("dbg_" + name), reads, [])

        for gc in range(NXS):
            load_x(gc)
        prepass_groups([0, 1, 2])
        for _ in range(NSLOT):
            ws_load()
        for c in range(TCH):
            h, R_h = norm_s1(c)
            norm_s2(c, 0, h, R_h)
        for t in range(ntiles):
            tap("hT", hT[:], [128, 8, 512], BF16, R_hT)
            load_cs(t, 0)
            phase_B(t)
            if t == 0:
                prepass_groups([3, 4])
            tap("qaT", qaT[:], [128, 8, 512], BF16, R_qaT)
            tap("kT", kT[:, 0, :, :], [128, 2, 640], BF16, [R_kTc, R_kTp])
            tap("vaug", vaug[:], [128, 5, 2, 65], BF16, R_vaug)
            phase_C(t)
            if t == 0:
                prepass_groups([5, 6, 7])
            tap("qrT", qrT[:], [128, 8, 512], BF16, R_qrT)
            tap("krT", krT[:], [128, 8, 512], BF16, R_krT)
            tap("big", big[:], [128, 12288], BF16, [R_big] + R_bigall)
            phase_DE(t)
            tap("attnT", attnT[:], [128, 8, 512], BF16, R_attnT)
            tap("retT", retT[:], [128, 16, 512], BF16, [R_retT])
            phase_F(t)
            phase_G(t)

        S.finalize()

        with contextlib.ExitStack() as es2:
            esems = {e: es2.enter_context(nc.semaphore(f"sem_{e}")) for e in Sched.ENGS}
            for dsm in S.dsems:
                dsm.handle = es2.enter_context(nc.semaphore(f"dsem_{dsm.name}"))
            block = es2.enter_context(nc.Block())

            @block.tensor
            def _(e):
                S.emit_engine("pe", e, esems)

            @block.scalar
            def _(e):
                S.emit_engine("act", e, esems)

            @block.vector
            def _(e):
                S.emit_engine("dve", e, esems)

            @block.gpsimd
            def _(e):
                S.emit_engine("pool", e, esems)
                for dsm in S.dsems:
                    if dsm.count and (dsm.name.startswith("o") or dsm.name.startswith("dbg_")):
                        e.wait_ge(dsm.handle, dsm.count)

            @block.sync
            def _(e):
                S.emit_engine("sp", e, esems)
    return nc


def host_consts(ntok):
    f32 = np.float32
    pos = np.arange(ntok, dtype=f32)
    theta = (1.0 / (np.float32(10000.0) ** np.linspace(0.0, 1.0, 128, dtype=f32))).astype(f32)
    ang = (pos[:, None] * theta[None, :]).astype(f32).astype(np.float64)
    cos = np.cos(ang).astype(f32)
    sin = np.sin(ang).astype(f32)
    cs = np.concatenate([cos, sin], axis=1).astype(f32)
    i = np.arange(128, dtype=np.float64)
    sqk = np.empty((128, 8), f32)
    for h in range(4):
        g = 1.0 - 2.0 ** (-5.0 - h)
        sqk[:, h] = g ** i
        sqk[:, 4 + h] = g ** (-i) / 16.0
    cf = np.zeros((128, 4, 128), f32)
    cf[:, 0, :] = np.eye(128, dtype=f32)
    bd = np.zeros((128, 128), f32)
    bd[:64, :64] = 1.0
    bd[64:, 64:] = 1.0
    cf[:, 1, :] = bd
    s = np.arange(128)[:, None]
    q = np.arange(128)[None, :]
    cf[:, 2, :] = (s <= q).astype(f32)
    cf[:, 3, :] = (s > q).astype(f32)
    return cs, sqk, cf


def make_in_maps(inputs, ntiles, ncores):
    ntok = ntiles * TCH * 128
    f = lambda a: np.ascontiguousarray(np.asarray(a, dtype=np.float32))
    cs, sqk, cf = host_consts(ntok)
    gT = np.concatenate([f(inputs["norm_mix_gain"])[0].reshape(8, 128).T,
                         f(inputs["norm_ffn_gain"])[0].reshape(8, 128).T], axis=1)
    qkg = np.stack([np.tile(f(inputs["q_norm_gain"])[0], 2), np.tile(f(inputs["k_norm_gain"])[0], 2)], axis=1)
    sinks = np.tile(f(inputs["attn_sinks"])[0][None, :], (128, 1))
    shared = {
        "w_in": f(inputs["w_in"])[0], "w_ba": f(inputs["w_branch_attn"])[0], "w_br": f(inputs["w_branch_ret"])[0],
        "w_o": f(inputs["w_out"])[0], "w_g": f(inputs["w_ffn_gate"])[0], "w_u": f(inputs["w_ffn_up"])[0],
        "w_d": f(inputs["w_ffn_down"])[0],
        "gT": f(gT), "qkg": f(qkg), "sinks": f(sinks), "cs": cs, "sqk": sqk, "cf": cf,
    }
    x = f(inputs["x"])
    maps = []
    for b in range(ncores):
        m = dict(shared)
        m["x"] = np.ascontiguousarray(x[b, :ntok, :])
        maps.append(m)
    return maps


def run(inputs, ntiles=16, ncores=NCORES):
    nc = build(ntiles)
    maps = make_in_maps(inputs, ntiles, ncores)
    res = run_bass_kernel_spmd(nc, maps, core_ids=list(range(ncores)))
    return np.stack([np.asarray(r["out"]) for r in res.results], axis=0)


def kernel(**inputs):
    out = run(inputs, 16, NCORES)
    return out.astype(np.float32)
```
